# Optimizing a Trainium2 kernel written in Bass

```python
import math
import jax
import jax.numpy as jnp
from jax import lax
import numpy as np

D_MODEL = 1024
BATCH = 2
SEQ = 8192
DEPTH = 2

RWKV_HEADS = 8
RWKV_HEAD_DIM = 64
RWKV_WIDTH = RWKV_HEADS * RWKV_HEAD_DIM
DECAY_LORA = 64
AAA_LORA = 64
GATE_LORA = 160
RWKV_SIZES = (RWKV_WIDTH, RWKV_WIDTH, RWKV_WIDTH, DECAY_LORA, AAA_LORA, GATE_LORA)
RWKV_COLS = sum(RWKV_SIZES)
RWKV_GN_EPS = 64e-5
LRU_WIDTH = 512
LRU_BLOCKS = 8
LRU_BLOCK_DIM = LRU_WIDTH // LRU_BLOCKS
CONV_WIDTH = 4
LRU_C = 8.0
ATTN_HEAD_DIM = 64
DILATION_PAIRS = ((128, 1), (512, 4), (2048, 16))
N_GROUPS = len(DILATION_PAIRS)
HEADS_PER_GROUP = 4
ATTN_HEADS = N_GROUPS * HEADS_PER_GROUP
ATTN_WIDTH = ATTN_HEADS * ATTN_HEAD_DIM
ATTN_OUT_WIDTH = HEADS_PER_GROUP * ATTN_HEAD_DIM
BLK = 128
N_BUCKETS = 32
MAX_DISTANCE = 2048
NEG_INF = -1e30
N_BRANCHES = 3
D_FF = 4 * D_MODEL
RMS_EPS = 1e-6
IN_SIZES = (RWKV_COLS, LRU_WIDTH, LRU_WIDTH, ATTN_WIDTH, ATTN_WIDTH, ATTN_WIDTH, N_BRANCHES * D_MODEL)
D_IN = sum(IN_SIZES)

kernel_name = 'hybrid_rwkv7_rglru_dilated_attn_block'


def split_cols(z, sizes):
    parts, off = [], 0
    for s in sizes:
        parts.append(z[..., off:off + s])
        off += s
    return parts


def rms_norm(x, g):
    xf = x.astype(jnp.float32)
    y = xf * lax.rsqrt(jnp.mean(xf * xf, axis=-1, keepdims=True) + RMS_EPS)
    return (y * g.astype(jnp.float32)).astype(x.dtype)


def token_shift(t):
    return jnp.pad(t, ((0, 0), (1, 0), (0, 0)))[:, :-1]


def rwkv7_scan(r, decay, k, v, a, b):
    bsz, _, nh, n = r.shape

    def step(state, inp):
        r_t, w_t, k_t, v_t, a_t, b_t = inp
        sa = jnp.einsum('bhij,bhj->bhi', state, a_t)
        state = (state * w_t[:, :, None, :] + sa[..., None] * b_t[:, :, None, :]
                 + v_t[..., None] * k_t[:, :, None, :])
        return state, jnp.einsum('bhij,bhj->bhi', state, r_t)

    seq_major = tuple(jnp.swapaxes(t, 0, 1) for t in (r, decay, k, v, a, b))
    _, y = lax.scan(step, jnp.zeros((bsz, nh, n, n), jnp.float32), seq_major)
    return jnp.swapaxes(y, 0, 1)


def rwkv7_mixer(feats, mu, w0, w_up, a0, a_up, g_up, k_k, k_a, r_k, ln_g, ln_b):
    f32 = jnp.float32
    f = feats.astype(f32)
    f = f + (token_shift(f) - f) * mu.astype(f32)
    r, k, v, xw, xa, xg = split_cols(f, RWKV_SIZES)
    bsz, seq, _ = r.shape
    w = -jax.nn.softplus(-(w0.astype(f32) + jnp.tanh(xw) @ w_up.astype(f32))) - 0.5
    decay = jnp.exp(-jnp.exp(w))
    a = jax.nn.sigmoid(a0.astype(f32) + xa @ a_up.astype(f32))
    g = jax.nn.sigmoid(xg) @ g_up.astype(f32)
    heads = lambda t: t.reshape(bsz, seq, RWKV_HEADS, RWKV_HEAD_DIM)
    kk = heads(k * k_k.astype(f32))
    kk = kk / jnp.maximum(jnp.sqrt(jnp.sum(kk * kk, axis=-1, keepdims=True)), 1e-12)
    k = k * (1.0 + (a - 1.0) * k_a.astype(f32))
    r_h, k_h, v_h, w_h, a_h = heads(r), heads(k), heads(v), heads(decay), heads(a)
    y = rwkv7_scan(r_h, w_h, k_h, v_h, -kk, kk * a_h)
    mean = jnp.mean(y, axis=-1, keepdims=True)
    var = jnp.mean(jnp.square(y - mean), axis=-1, keepdims=True)
    y = ((y - mean) * lax.rsqrt(var + RWKV_GN_EPS)).reshape(bsz, seq, RWKV_WIDTH)
    y = y * ln_g.astype(f32) + ln_b.astype(f32)
    bonus = jnp.sum(r_h * k_h * r_k.astype(f32), axis=-1, keepdims=True) * v_h
    return (y + bonus.reshape(bsz, seq, RWKV_WIDTH)) * g


def linear_scan(a, b):
    def combine(left, right):
        a_l, b_l = left
        a_r, b_r = right
        return a_l * a_r, a_r * b_l + b_r
    _, h = lax.associative_scan(combine, (a, b), axis=1)
    return h


def rglru_mixer(xb, yb, conv_w, conv_b, wa, ba, wx, bx, lam):
    f32 = jnp.float32
    xb = xb.astype(f32)
    bsz, seq, _ = xb.shape
    xc = lax.conv_general_dilated(xb, conv_w.astype(f32)[:, None, :], window_strides=(1,),
                                  padding=((CONV_WIDTH - 1, 0),),
                                  dimension_numbers=('NWC', 'WIO', 'NWC'),
                                  feature_group_count=LRU_WIDTH) + conv_b.astype(f32)
    xblk = xc.reshape(bsz, seq, LRU_BLOCKS, LRU_BLOCK_DIM)
    gate_a = jax.nn.sigmoid(jnp.einsum('bsni,nij->bsnj', xblk, wa.astype(f32)).reshape(bsz, seq, LRU_WIDTH) + ba.astype(f32))
    gate_x = jax.nn.sigmoid(jnp.einsum('bsni,nij->bsnj', xblk, wx.astype(f32)).reshape(bsz, seq, LRU_WIDTH) + bx.astype(f32))
    log_a = -LRU_C * gate_a * jax.nn.softplus(-lam.astype(f32))
    a = jnp.exp(log_a)
    mult = jnp.sqrt(jnp.maximum(-jnp.expm1(2.0 * log_a), 0.0))
    mult = jnp.where((jnp.arange(seq) == 0)[None, :, None], 1.0, mult)
    h = linear_scan(a, xc * gate_x * mult)
    return h * jax.nn.gelu(yb.astype(f32), approximate=True)


def t5_bucket(dist):
    max_exact = N_BUCKETS // 2
    d = jnp.maximum(dist, 0)
    large = max_exact + (jnp.log(jnp.maximum(d, 1).astype(jnp.float32) / max_exact)
                         / math.log(MAX_DISTANCE / max_exact) * (N_BUCKETS - max_exact)).astype(jnp.int32)
    large = jnp.minimum(large, N_BUCKETS - 1)
    return jnp.where(d < max_exact, d, large)


def dilated_group_attention(q, k, v, bias_table, window, dilation):
    bsz, seq, hg, dh = q.shape
    span = dilation * BLK
    s_pad = -(-seq // span) * span
    nb = s_pad // span

    def to_sub(t):
        t = jnp.pad(t, ((0, 0), (0, s_pad - seq), (0, 0), (0, 0)))
        return t.reshape(bsz, nb, BLK, dilation, hg, dh).transpose(0, 3, 1, 2, 4, 5)

    def with_prev(t):
        prev = jnp.pad(t, ((0, 0), (0, 0), (1, 0), (0, 0), (0, 0), (0, 0)))[:, :, :-1]
        return jnp.concatenate([prev, t], axis=3)

    qs = to_sub(q)
    kb, vb = with_prev(to_sub(k)), with_prev(to_sub(v))
    n_keys = window // dilation
    kj = jnp.arange(2 * BLK)[None, :]
    rel = (jnp.arange(BLK)[:, None] + BLK) - kj
    band = (rel >= 0) & (rel <= n_keys)
    valid = band[None] & ((jnp.arange(nb)[:, None, None] > 0) | (kj >= BLK)[None])
    bias = jnp.moveaxis(bias_table.astype(jnp.float32)[t5_bucket(rel * dilation)], -1, 0)
    logits = jnp.einsum('brnqhe,brnkhe->brnhqk', qs, kb) + bias
    logits = jnp.where(valid[None, None, :, None], logits, NEG_INF)
    m = jnp.max(logits, axis=-1, keepdims=True)
    p = jnp.exp(logits - m)
    l = jnp.sum(p, axis=-1)
    o = jnp.einsum('brnhqk,brnkhe->brnqhe', p, vb) / jnp.swapaxes(l, -1, -2)[..., None]
    lse = jnp.swapaxes(m[..., 0] + jnp.log(l), -1, -2)

    def from_sub(t):
        t = jnp.moveaxis(t, 1, 3)
        return t.reshape((bsz, s_pad) + t.shape[4:])[:, :seq]

    return from_sub(o), from_sub(lse)


def dilated_attention_mixer(zq, zk, zv, q_g, k_g, rel_bias):
    f32 = jnp.float32
    bsz, seq, _ = zq.shape
    shp = (bsz, seq, ATTN_HEADS, ATTN_HEAD_DIM)
    q = rms_norm(zq.astype(f32).reshape(shp), q_g) * (ATTN_HEAD_DIM ** -0.5)
    k = rms_norm(zk.astype(f32).reshape(shp), k_g)
    v = zv.astype(f32).reshape(shp)
    outs, lses = [], []
    for gi, (window, dil) in enumerate(DILATION_PAIRS):
        hs = slice(gi * HEADS_PER_GROUP, (gi + 1) * HEADS_PER_GROUP)
        o, lse = dilated_group_attention(q[:, :, hs], k[:, :, hs], v[:, :, hs], rel_bias[:, hs], window, dil)
        outs.append(o)
        lses.append(lse)
    wts = jax.nn.softmax(jnp.stack(lses, axis=0), axis=0)
    o = jnp.sum(jnp.stack(outs, axis=0) * wts[..., None], axis=0)
    return o.reshape(bsz, seq, ATTN_OUT_WIDTH)


def setup_inputs(seed: int = 0) -> dict:
    key = jax.random.key(seed)
    ks = jax.random.split(key, 32)
    L = DEPTH
    nrm = lambda kk, shape, s: jax.random.normal(kk, shape, jnp.float32) * s
    u = jax.random.uniform(ks[22], (L, LRU_WIDTH), jnp.float32, 0.9, 0.999)
    s = u ** (1.0 / LRU_C)
    return {
        'x': nrm(ks[0], (BATCH, SEQ, D_MODEL), 1.0),
        'rel_bias': nrm(ks[1], (N_BUCKETS, ATTN_HEADS), 0.5),
        'norm_mix_g': 1.0 + nrm(ks[2], (L, D_MODEL), 0.02),
        'w_in': nrm(ks[3], (L, D_MODEL, D_IN), D_MODEL ** -0.5),
        'rwkv_mu': jax.random.uniform(ks[4], (L, RWKV_COLS), jnp.float32),
        'rwkv_w0': jax.random.uniform(ks[5], (L, RWKV_WIDTH), jnp.float32, -6.0, -1.0),
        'rwkv_w_up': nrm(ks[6], (L, DECAY_LORA, RWKV_WIDTH), 0.1 * DECAY_LORA ** -0.5),
        'rwkv_a0': nrm(ks[7], (L, RWKV_WIDTH), 0.1),
        'rwkv_a_up': nrm(ks[8], (L, AAA_LORA, RWKV_WIDTH), AAA_LORA ** -0.5),
        'rwkv_g_up': nrm(ks[9], (L, GATE_LORA, RWKV_WIDTH), GATE_LORA ** -0.5),
        'rwkv_k_k': 0.85 + nrm(ks[10], (L, RWKV_WIDTH), 0.02),
        'rwkv_k_a': 1.0 + nrm(ks[11], (L, RWKV_WIDTH), 0.02),
        'rwkv_r_k': nrm(ks[12], (L, RWKV_HEADS, RWKV_HEAD_DIM), 0.1),
        'rwkv_ln_g': 1.0 + nrm(ks[13], (L, RWKV_WIDTH), 0.02),
        'rwkv_ln_b': nrm(ks[14], (L, RWKV_WIDTH), 0.02),
        'proj_a': nrm(ks[15], (L, RWKV_WIDTH, D_MODEL), RWKV_WIDTH ** -0.5),
        'conv_w': nrm(ks[16], (L, CONV_WIDTH, LRU_WIDTH), CONV_WIDTH ** -0.5),
        'conv_b': nrm(ks[17], (L, LRU_WIDTH), 0.02),
        'lru_wa': nrm(ks[18], (L, LRU_BLOCKS, LRU_BLOCK_DIM, LRU_BLOCK_DIM), LRU_BLOCK_DIM ** -0.5),
        'lru_ba': nrm(ks[19], (L, LRU_WIDTH), 0.02),
        'lru_wx': nrm(ks[20], (L, LRU_BLOCKS, LRU_BLOCK_DIM, LRU_BLOCK_DIM), LRU_BLOCK_DIM ** -0.5),
        'lru_bx': nrm(ks[21], (L, LRU_WIDTH), 0.02),
        'lru_lambda': jnp.log(s) - jnp.log1p(-s),
        'proj_b': nrm(ks[23], (L, LRU_WIDTH, D_MODEL), LRU_WIDTH ** -0.5),
        'q_norm_g': 1.0 + nrm(ks[24], (L, ATTN_HEAD_DIM), 0.02),
        'k_norm_g': 1.0 + nrm(ks[25], (L, ATTN_HEAD_DIM), 0.02),
        'proj_c': nrm(ks[26], (L, ATTN_OUT_WIDTH, D_MODEL), ATTN_OUT_WIDTH ** -0.5),
        'w_out': nrm(ks[27], (L, D_MODEL, D_MODEL), D_MODEL ** -0.5),
        'norm_mlp_g': 1.0 + nrm(ks[28], (L, D_MODEL), 0.02),
        'mlp_up': nrm(ks[29], (L, D_MODEL, D_FF), D_MODEL ** -0.5),
        'mlp_down': nrm(ks[30], (L, D_FF, D_MODEL), D_FF ** -0.5),
    }


def reference(x, rel_bias, norm_mix_g, w_in, rwkv_mu, rwkv_w0, rwkv_w_up, rwkv_a0, rwkv_a_up,
              rwkv_g_up, rwkv_k_k, rwkv_k_a, rwkv_r_k, rwkv_ln_g, rwkv_ln_b, proj_a, conv_w, conv_b,
              lru_wa, lru_ba, lru_wx, lru_bx, lru_lambda, proj_b, q_norm_g, k_norm_g, proj_c, w_out,
              norm_mlp_g, mlp_up, mlp_down):
    dt = x.dtype
    bsz, seq, _ = x.shape
    for l in range(DEPTH):
        h = rms_norm(x, norm_mix_g[l])
        z_a, z_bx, z_by, z_q, z_k, z_v, z_g = split_cols(h @ w_in[l], IN_SIZES)
        y_a = rwkv7_mixer(z_a, rwkv_mu[l], rwkv_w0[l], rwkv_w_up[l], rwkv_a0[l], rwkv_a_up[l],
                          rwkv_g_up[l], rwkv_k_k[l], rwkv_k_a[l], rwkv_r_k[l], rwkv_ln_g[l],
                          rwkv_ln_b[l]).astype(dt) @ proj_a[l]
        y_b = rglru_mixer(z_bx, z_by, conv_w[l], conv_b[l], lru_wa[l], lru_ba[l], lru_wx[l],
                          lru_bx[l], lru_lambda[l]).astype(dt) @ proj_b[l]
        y_c = dilated_attention_mixer(z_q, z_k, z_v, q_norm_g[l], k_norm_g[l],
                                      rel_bias).astype(dt) @ proj_c[l]
        gates = jax.nn.sigmoid(z_g.reshape(bsz, seq, N_BRANCHES, D_MODEL))
        merged = gates[:, :, 0] * y_a + gates[:, :, 1] * y_b + gates[:, :, 2] * y_c
        x = x + merged @ w_out[l]
        u = rms_norm(x, norm_mlp_g[l]) @ mlp_up[l]
        x = x + jnp.square(jax.nn.relu(u)) @ mlp_down[l]
    return x
```

```python
import contextlib
import os
import numpy as np
import concourse.bass as bass
import concourse.mybir as mybir
from concourse.bass_utils import run_bass_kernel_spmd

F32 = mybir.dt.float32
BF16 = mybir.dt.bfloat16
AF = mybir.ActivationFunctionType
ALU = mybir.AluOpType
AX = mybir.AxisListType

NCORES = 8
D = 1024
SEQ = 8192
NT = 2048
TT = 512
NTT = NT // TT
MIX_COLS = 5152
GATE_COLS = 3072
D_IN = 8224
D_FF = 4096
RMS_EPS = 1e-6
GN_EPS = 64e-5
CH = 64


class Buf:
    __slots__ = ("name", "w", "r", "multi", "wl")

    def __init__(self, name="", multi=False):
        self.name = name
        self.w = None
        self.r = {}
        self.multi = multi
        self.wl = {}


class V:
    __slots__ = ("ap", "b")

    def __init__(self, ap, b):
        self.ap = ap
        self.b = b


class T:
    def __init__(self, t, name, multi=False):
        self.t = t
        self.b = Buf(name, multi)

    def __getitem__(self, idx):
        return V(self.t[idx], self.b)

    def v(self, ap):
        return V(ap, self.b)


class Ctx:
    NDMA = 16

    def __init__(self, nc):
        self.nc = nc
        self.eng = {"pe": nc.tensor, "act": nc.scalar, "dve": nc.vector,
                    "pool": nc.gpsimd, "sp": nc.sync}
        self.sems = {}
        self.cnt = {}
        for k in self.eng:
            self.sems[k] = nc.alloc_semaphore("sem_" + k)
            self.cnt[k] = 0
        self.waited = {k: {} for k in self.eng}
        self.dq = {}
        for q in ("sp", "act"):
            lst = []
            for i in range(self.NDMA):
                key = "d_%s_%d" % (q, i)
                self.sems[key] = nc.alloc_semaphore(key)
                self.cnt[key] = 0
                lst.append(key)
            self.dq[q] = [lst, 0]
        self.n_instr = 0
        self.uid = 0
        self.stacks = []

    @contextlib.contextmanager
    def scope(self):
        st = contextlib.ExitStack()
        self.stacks.append(st)
        try:
            yield
        finally:
            self.stacks.pop()
            self.barrier()
            st.close()

    def barrier(self):
        evs = [(k, c) for k, c in self.cnt.items() if c > 0]
        for ek in self.eng:
            self._wait(ek, evs)

    def sb(self, shape, dtype, name=None):
        self.uid += 1
        name = "%s_%d" % (name or "sb", self.uid)
        if self.stacks:
            t = self.stacks[-1].enter_context(self.nc.sbuf_tensor(name, list(shape), dtype))
        else:
            t = self.nc.alloc_sbuf_tensor(name, list(shape), dtype)
        return T(t, name)

    def ps(self, shape, dtype=F32, name=None):
        self.uid += 1
        name = "%s_%d" % (name or "ps", self.uid)
        if self.stacks:
            t = self.stacks[-1].enter_context(self.nc.psum_tensor(name, list(shape), dtype))
        else:
            t = self.nc.alloc_psum_tensor(name, list(shape), dtype)
        return T(t, name)

    def dram(self, name, shape, dtype, kind):
        return T(self.nc.dram_tensor(name, list(shape), dtype, kind=kind).ap(), name, multi=True)

    def _wait(self, ek, deps):
        eng = self.eng[ek]
        best = {}
        for (sk, v) in deps:
            if v > best.get(sk, 0):
                best[sk] = v
        for sk, v in best.items():
            if ek == "pe" and sk == "pe":
                continue
            if self.waited[ek].get(sk, 0) >= v:
                continue
            eng.wait_ge(self.sems[sk], v)
            self.waited[ek][sk] = v
            self.n_instr += 1

    @staticmethod
    def _deps(reads, writes):
        deps = []
        for b in reads:
            if b.w is not None:
                deps.append(b.w)
        for b in writes:
            if b.multi:
                continue
            if b.w is not None:
                deps.append(b.w)
            deps.extend(b.r.items())
        return deps

    @staticmethod
    def _mark(ev, reads, writes):
        sk, v = ev
        for b in reads:
            if b.r.get(sk, 0) < v:
                b.r[sk] = v
        for b in writes:
            if b.multi:
                if b.wl.get(sk, 0) < v:
                    b.wl[sk] = v
                continue
            b.w = ev
            b.r = {}

    def _split(self, kw):
        reads, writes, args = [], [], {}
        for k, a in kw.items():
            if isinstance(a, V):
                (writes if k in ("out", "accum_out", "ap") else reads).append(a.b)
                args[k] = a.ap
            else:
                args[k] = a
        return reads, writes, args

    def I(self, ek, meth, **kw):
        reads, writes, args = self._split(kw)
        self._wait(ek, self._deps(reads, writes))
        ins = getattr(self.eng[ek], meth)(**args)
        ins.then_inc(self.sems[ek], 1)
        self.cnt[ek] += 1
        self.n_instr += 1
        self._mark((ek, self.cnt[ek]), reads, writes)
        return ins

    def dma(self, out, in_, q="sp", **kw):
        lst, rr = self.dq[q]
        sk = lst[rr % len(lst)]
        self.dq[q][1] = rr + 1
        reads, writes = [in_.b], [out.b]
        deps = self._deps(reads, writes)
        if self.cnt[sk] > 0:
            deps.append((sk, self.cnt[sk]))
        self._wait(q, deps)
        ins = self.eng[q].dma_start(out=out.ap, in_=in_.ap, **kw)
        ins.then_inc(self.sems[sk], 16)
        self.cnt[sk] += 16
        self.n_instr += 1
        self._mark((sk, self.cnt[sk]), reads, writes)
        return ins

    def finish(self, tiles):
        deps = []
        for t in tiles:
            if t.b.w is not None:
                deps.append(t.b.w)
            deps.extend(t.b.wl.items())
        self._wait("sp", deps)

    def mm(self, out, lhsT, rhs, start=True, stop=True, **kw):
        return self.I("pe", "matmul", out=out, lhsT=lhsT, rhs=rhs, start=start, stop=stop, **kw)

    def act(self, out, in_, func, eng="act", **kw):
        return self.I("act", "activation", out=out, in_=in_, func=func, **kw)

    def tt(self, out, in0, in1, op, eng="dve"):
        return self.I(eng, "tensor_tensor", out=out, in0=in0, in1=in1, op=op)

    def ts(self, out, in0, s1, op0, s2=None, op1=None, eng="dve"):
        if op1 is None:
            return self.I(eng, "tensor_scalar", out=out, in0=in0, scalar1=s1, scalar2=None, op0=op0)
        return self.I(eng, "tensor_scalar", out=out, in0=in0, scalar1=s1, scalar2=s2, op0=op0, op1=op1)

    def stt(self, out, in0, scalar, in1, op0, op1):
        return self.I("dve", "scalar_tensor_tensor", out=out, in0=in0, scalar=scalar, in1=in1,
                      op0=op0, op1=op1)

    def copy(self, out, in_, eng="dve"):
        if eng == "act":
            return self.I("act", "activation", out=out, in_=in_, func=AF.Copy)
        return self.I(eng, "tensor_copy", out=out, in_=in_)

    def memset(self, ap, val, eng="pool"):
        return self.I(eng, "memset", ap=ap, constant=val)


def new_ctx():
    nc = bass.Bass("TRN2", target_bir_lowering=False)
    return nc, Ctx(nc)


def run(nc, in_maps):
    res = run_bass_kernel_spmd(nc, in_maps, core_ids=list(range(NCORES)))
    return res.results


def col_load(ctx, dst, src_ap_T, n, q="sp"):
    if n <= 128:
        ctx.dma(dst[0:n, 0:1], src_ap_T.v(src_ap_T.t.rearrange("(p o) -> p o", o=1)), q=q)
    else:
        ctx.dma(dst[:, 0:n // 128], src_ap_T.v(src_ap_T.t.rearrange("(k p) -> p k", p=128)), q=q,
                allow_slow_non_contiguous=True)


def rmsnorm_T(ctx, K, xT, gcol, hts, ps_list, sq_list, rstd):
    for tt in range(NTT):
        ts = slice(tt * TT, (tt + 1) * TT)
        ps = ps_list[tt % len(ps_list)]
        for k in range(8):
            sq = sq_list[k % len(sq_list)]
            ctx.act(sq[:, :], xT[:, k, ts], AF.Square)
            ctx.mm(ps[:, :], K["ones_f"][:, :], sq[:, :], start=(k == 0), stop=(k == 7))
        ctx.act(rstd[:, :], ps[:, :], AF.Sqrt, bias=K["eps"][:, 0:1], scale=1.0 / D)
        ctx.I("dve", "reciprocal", out=rstd[:, :], in_=rstd[:, :])
        for k in range(8):
            ctx.stt(hts[tt][:, k, :], xT[:, k, ts], gcol[:, k:k + 1], rstd[:, :], ALU.mult, ALU.mult)


def consts(ctx):
    K = {}
    K["ones_f"] = ctx.sb([128, 128], F32, "ones_f")
    ctx.memset(K["ones_f"][:, :], 1.0)
    K["eps"] = ctx.sb([128, 1], F32, "eps")
    ctx.memset(K["eps"][:, :], RMS_EPS)
    return K


def phase_A(ctx, K, xT, d_win, d_g, d_zT, d_gT, sbufs=None):
    gcol = ctx.sb([128, 8], F32, "gcolA")
    col_load(ctx, gcol, d_g, D)
    hts = [ctx.sb([128, 8, TT], BF16, "hT") for _ in range(NTT)]
    ps_n = [ctx.ps([128, TT], F32, "psn") for _ in range(2)]
    sq_list = [ctx.sb([128, TT], F32, "sq") for _ in range(2)]
    rstd = ctx.sb([128, TT], F32, "rstd")
    rmsnorm_T(ctx, K, xT, gcol, hts, ps_n, sq_list, rstd)

    ps_g = [ctx.ps([128, TT], F32, "psg") for _ in range(4)]
    wst = [ctx.sb([128, 8, 512], F32, "wst") for _ in range(2)]
    wbf = [ctx.sb([128, 8, 512], BF16, "wbf") for _ in range(2)]
    ost = [ctx.sb([128, TT], F32, "ost") for _ in range(4)]
    blocks = []
    c = 0
    while c < MIX_COLS:
        n = min(512, MIX_COLS - c)
        blocks.append((c, n, False, c))
        c += n
    c = 0
    while c < GATE_COLS:
        blocks.append((MIX_COLS + c, 512, True, c))
        c += 512
    wv = d_win.t.rearrange("(k p) c -> p k c", p=128)

    def load(bi):
        c0, n, _, _ = blocks[bi]
        ctx.dma(wst[bi % 2][:, :, 0:n], d_win.v(wv[:, :, c0:c0 + n]))

    load(0)
    cnt = 0
    for bi, (c0, n, is_gate, r0) in enumerate(blocks):
        if bi + 1 < len(blocks):
            load(bi + 1)
        w32, w16 = wst[bi % 2], wbf[bi % 2]
        for k in range(8):
            ctx.copy(w16[:, k, 0:n], w32[:, k, 0:n], eng=("pool" if k % 2 == 0 else "dve"))
        for ci in range(0, n, 128):
            m = min(128, n - ci)
            for tt in range(NTT):
                ps = ps_g[cnt % 4]
                o = ost[cnt % 4]
                for k in range(8):
                    ctx.mm(ps[0:m, :], w16[:, k, ci:ci + m], hts[tt][:, k, :], start=(k == 0), stop=(k == 7))
                if is_gate:
                    ctx.act(o[0:m, :], ps[0:m, :], AF.Sigmoid)
                    dst = d_gT
                elif cnt % 2 == 0:
                    ctx.copy(o[0:m, :], ps[0:m, :], eng="dve")
                    dst = d_zT
                else:
                    ctx.copy(o[0:m, :], ps[0:m, :], eng="act")
                    dst = d_zT
                ctx.dma(dst[r0 + ci:r0 + ci + m, tt * TT:(tt + 1) * TT], o[0:m, :])
                cnt += 1


def build_A():
    nc, ctx = new_ctx()
    d_xT = ctx.dram("xT", [D, NT], F32, "ExternalInput")
    d_win = ctx.dram("w_in", [D, D_IN], F32, "ExternalInput")
    d_g = ctx.dram("g", [D], F32, "ExternalInput")
    d_zT = ctx.dram("zT", [MIX_COLS, NT], F32, "ExternalOutput")
    d_gT = ctx.dram("gT", [GATE_COLS, NT], F32, "ExternalOutput")
    K = consts(ctx)
    xT = ctx.sb([128, 8, NT], F32, "xT")
    xv = d_xT.t.rearrange("(k p) t -> p k t", p=128)
    for k in range(8):
        ctx.dma(xT[:, k, :], d_xT.v(xv[:, k, :]))
    phase_A(ctx, K, xT, d_win, d_g, d_zT, d_gT)
    ctx.finish([d_zT, d_gT])
    return nc


def load_weight_bf16(ctx, dst, d_w, kchunks, ncols, stage, col0=0, row0=0):
    for k in range(kchunks):
        st = stage[k % len(stage)]
        ctx.dma(st[:, 0:ncols], d_w[row0 + k * 128:row0 + (k + 1) * 128, col0:col0 + ncols])
        ctx.copy(dst[:, k, :], st[:, 0:ncols], eng=("pool" if k % 2 == 0 else "dve"))


def phase_C(ctx, K, xT, d_y, d_gT, d_w, d_norm_g):
    with ctx.scope():
        _phase_C_merge(ctx, K, xT, d_y, d_gT, d_w)
    with ctx.scope():
        _phase_C_mlp(ctx, K, xT, d_w, d_norm_g)


def _phase_C_merge(ctx, K, xT, d_y, d_gT, d_w):
    stage = [ctx.sb([128, 1024], F32, "stgC") for _ in range(2)]
    pa = ctx.sb([128, 4, D], BF16, "pa")
    pb = ctx.sb([128, 4, D], BF16, "pb")
    pc = ctx.sb([128, 2, D], BF16, "pc")
    wo = ctx.sb([128, 8, D], BF16, "wo")
    load_weight_bf16(ctx, pa, d_w["proj_a"], 4, D, stage)
    load_weight_bf16(ctx, pb, d_w["proj_b"], 4, D, stage)
    load_weight_bf16(ctx, pc, d_w["proj_c"], 2, D, stage)
    load_weight_bf16(ctx, wo, d_w["w_out"], 8, D, stage)
    ps = [ctx.ps([128, TT], F32, "psC") for _ in range(6)]
    pi = [0]

    def nps():
        pi[0] += 1
        return ps[pi[0] % 6]

    ybf = {"a": ctx.sb([128, 4, TT], BF16, "yabf"), "b": ctx.sb([128, 4, TT], BF16, "ybbf"),
           "c": ctx.sb([128, 2, TT], BF16, "ycbf")}
    yst = [ctx.sb([128, TT], F32, "yst") for _ in range(3)]
    gst = [ctx.sb([128, TT], F32, "gst") for _ in range(3)]
    m1 = [ctx.sb([128, TT], F32, "m1") for _ in range(2)]
    m2 = [ctx.sb([128, TT], F32, "m2") for _ in range(2)]
    mbf = ctx.sb([128, 8, TT], BF16, "mbf")
    cnt = 0
    for tt in range(NTT):
        ts = slice(tt * TT, (tt + 1) * TT)
        for key, dk, nk in (("a", "yA", 4), ("b", "yB", 4), ("c", "yC", 2)):
            for k in range(nk):
                st = yst[cnt % 3]
                cnt += 1
                ctx.dma(st[:, :], d_y[dk][k * 128:(k + 1) * 128, ts])
                ctx.copy(ybf[key][:, k, :], st[:, :], eng="pool")
        for m in range(8):
            ms = slice(m * 128, (m + 1) * 128)
            pss = []
            for key, w, nk in (("a", pa, 4), ("b", pb, 4), ("c", pc, 2)):
                p = nps()
                for k in range(nk):
                    ctx.mm(p[:, :], w[:, k, ms], ybf[key][:, k, :], start=(k == 0), stop=(k == nk - 1))
                pss.append(p)
            gs = []
            for br in range(3):
                g = gst[br]
                ctx.dma(g[:, :], d_gT[br * D + m * 128:br * D + (m + 1) * 128, ts])
                gs.append(g)
            a1, a2 = m1[m % 2], m2[m % 2]
            ctx.tt(a1[:, :], pss[0][:, :], gs[0][:, :], ALU.mult)
            ctx.tt(a2[:, :], pss[1][:, :], gs[1][:, :], ALU.mult)
            ctx.tt(a1[:, :], a1[:, :], a2[:, :], ALU.add, eng="pool")
            ctx.tt(a2[:, :], pss[2][:, :], gs[2][:, :], ALU.mult)
            ctx.tt(mbf[:, m, :], a1[:, :], a2[:, :], ALU.add, eng="pool")
        for m in range(8):
            p = nps()
            for k in range(8):
                ctx.mm(p[:, :], wo[:, k, m * 128:(m + 1) * 128], mbf[:, k, :], start=(k == 0), stop=(k == 7))
            ctx.tt(xT[:, m, ts], xT[:, m, ts], p[:, :], ALU.add)


def _phase_C_mlp(ctx, K, xT, d_w, d_norm_g):
    ps = [ctx.ps([128, TT], F32, "psM") for _ in range(6)]
    pi = [0]

    def nps():
        pi[0] += 1
        return ps[pi[0] % 6]

    gcol = ctx.sb([128, 8], F32, "gcolC")
    col_load(ctx, gcol, d_norm_g, D)
    hts = [ctx.sb([128, 8, TT], BF16, "hT2") for _ in range(NTT)]
    sq_list = [ctx.sb([128, TT], F32, "sq2") for _ in range(2)]
    rstd = ctx.sb([128, TT], F32, "rstd2")
    rmsnorm_T(ctx, K, xT, gcol, hts, ps[0:2], sq_list, rstd)
    FB = 256
    nfb = D_FF // FB
    upst = [ctx.sb([128, 8, FB], F32, "upst") for _ in range(2)]
    upbf = [ctx.sb([128, 8, FB], BF16, "upbf") for _ in range(2)]
    dnst = [ctx.sb([128, FB // 128, D], F32, "dnst") for _ in range(2)]
    dnbf = [ctx.sb([128, FB // 128, D], BF16, "dnbf") for _ in range(2)]
    actb = [ctx.sb([128, FB // 128, NT], BF16, "actb") for _ in range(2)]
    upv = d_w["mlp_up"].t.rearrange("(k p) f -> p k f", p=128)
    dnv = d_w["mlp_down"].t.rearrange("(c p) m -> p c m", p=128)

    def loadw(fb):
        ctx.dma(upst[fb % 2][:, :, :], d_w["mlp_up"].v(upv[:, :, fb * FB:(fb + 1) * FB]))
        ctx.dma(dnst[fb % 2][:, :, :], d_w["mlp_down"].v(dnv[:, fb * (FB // 128):(fb + 1) * (FB // 128), :]))

    loadw(0)
    for fb in range(nfb):
        if fb + 1 < nfb:
            loadw(fb + 1)
        u16, d16, ab = upbf[fb % 2], dnbf[fb % 2], actb[fb % 2]
        for k in range(8):
            ctx.copy(u16[:, k, :], upst[fb % 2][:, k, :], eng=("pool" if k % 2 == 0 else "dve"))
        for c in range(FB // 128):
            ctx.copy(d16[:, c, :], dnst[fb % 2][:, c, :], eng="pool")
        for tt in range(NTT):
            ts = slice(tt * TT, (tt + 1) * TT)
            for c in range(FB // 128):
                p = nps()
                for k in range(8):
                    ctx.mm(p[:, :], u16[:, k, c * 128:(c + 1) * 128], hts[tt][:, k, :], start=(k == 0), stop=(k == 7))
                r = sq_list[c % 2]
                ctx.act(r[:, :], p[:, :], AF.Relu)
                ctx.tt(ab[:, c, ts], r[:, :], r[:, :], ALU.mult, eng="pool")
        for tt in range(NTT):
            ts = slice(tt * TT, (tt + 1) * TT)
            for m in range(8):
                p = nps()
                nc_ = FB // 128
                for c in range(nc_):
                    ctx.mm(p[:, :], d16[:, c, m * 128:(m + 1) * 128], ab[:, c, ts], start=(c == 0), stop=(c == nc_ - 1))
                ctx.tt(xT[:, m, ts], xT[:, m, ts], p[:, :], ALU.add)


def build_C():
    nc, ctx = new_ctx()
    d_xT = ctx.dram("xT", [D, NT], F32, "ExternalInput")
    d_y = {"yA": ctx.dram("yA", [512, NT], F32, "ExternalInput"),
           "yB": ctx.dram("yB", [512, NT], F32, "ExternalInput"),
           "yC": ctx.dram("yC", [256, NT], F32, "ExternalInput")}
    d_gT = ctx.dram("gT", [GATE_COLS, NT], F32, "ExternalInput")
    d_w = {"proj_a": ctx.dram("proj_a", [512, D], F32, "ExternalInput"),
           "proj_b": ctx.dram("proj_b", [512, D], F32, "ExternalInput"),
           "proj_c": ctx.dram("proj_c", [256, D], F32, "ExternalInput"),
           "w_out": ctx.dram("w_out", [D, D], F32, "ExternalInput"),
           "mlp_up": ctx.dram("mlp_up", [D, D_FF], F32, "ExternalInput"),
           "mlp_down": ctx.dram("mlp_down", [D_FF, D], F32, "ExternalInput")}
    d_ng = ctx.dram("norm_mlp_g", [D], F32, "ExternalInput")
    d_out = ctx.dram("xT_out", [D, NT], F32, "ExternalOutput")
    K = consts(ctx)
    xT = ctx.sb([128, 8, NT], F32, "xT")
    xv = d_xT.t.rearrange("(k p) t -> p k t", p=128)
    for k in range(8):
        ctx.dma(xT[:, k, :], d_xT.v(xv[:, k, :]))
    phase_C(ctx, K, xT, d_y, d_gT, d_w, d_ng)
    ov = d_out.t.rearrange("(k p) t -> p k t", p=128)
    for k in range(8):
        ctx.dma(d_out.v(ov[:, k, :]), xT[:, k, :])
    ctx.finish([d_out])
    return nc


def pcol(ctx, d_vec, n=128, name="pc"):
    t = ctx.sb([128, 1], F32, name)
    ctx.dma(t[0:n, 0:1], d_vec.v(d_vec.t.rearrange("(p o) -> p o", o=1)))
    return t


def mixer_lru(ctx, K, d_zx, d_zy, d_p, d_out):
    TB = 2048
    cw = ctx.sb([128, 4], F32, "cw")
    ctx.dma(cw[:, :], d_p["conv_w"].v(d_p["conv_w"].t.rearrange("w c -> c w")), allow_slow_non_contiguous=True)
    cb = pcol(ctx, d_p["conv_b"])
    ba = pcol(ctx, d_p["lru_ba"])
    bx = pcol(ctx, d_p["lru_bx"])
    lam = pcol(ctx, d_p["lru_lambda"])
    c1 = ctx.sb([128, 1], F32, "c1")
    ctx.act(c1[:, :], lam[:, :], AF.Exp, scale=-1.0)
    ctx.act(c1[:, :], c1[:, :], AF.Ln, bias=K["ones_f"][:, 0:1], scale=1.0)
    ctx.ts(c1[:, :], c1[:, :], -8.0, ALU.mult)
    wst = ctx.sb([128, 128], F32, "lruwst")
    wa = ctx.sb([128, 128], BF16, "lruwa")
    wx = ctx.sb([128, 128], BF16, "lruwx")
    for dst, key in ((wa, "lru_wa"), (wx, "lru_wx")):
        ctx.memset(wst[:, :], 0.0)
        for n in range(2):
            ctx.dma(wst[n * 64:(n + 1) * 64, n * 64:(n + 1) * 64], d_p[key][n, :, :])
        ctx.copy(dst[:, :], wst[:, :])
    xb = ctx.sb([128, 3 + TB], F32, "lru_xb")
    yb = ctx.sb([128, TB], F32, "lru_yb")
    xc = ctx.sb([128, TB], F32, "lru_xc")
    xcb = ctx.sb([128, TB], BF16, "lru_xcb")
    ga = ctx.sb([128, TB], F32, "lru_ga")
    gx = ctx.sb([128, TB], F32, "lru_gx")
    t1 = ctx.sb([128, TB], F32, "lru_t1")
    hh = [ctx.sb([128, TB], F32, "lru_h") for _ in range(2)]
    ps = [ctx.ps([128, 512], F32, "lru_ps") for _ in range(4)]
    pi = 0
    ctx.memset(xb[:, 0:3], 0.0)
    for blk in range(SEQ // TB):
        t0 = blk * TB
        if blk > 0:
            ctx.copy(xb[:, 0:3], xb[:, TB:TB + 3])
        ctx.dma(xb[:, 3:3 + TB], d_zx[:, t0:t0 + TB])
        ctx.dma(yb[:, :], d_zy[:, t0:t0 + TB])
        ctx.ts(xc[:, :], xb[:, 3:3 + TB], cw[:, 3:4], ALU.mult, cb[:, 0:1], ALU.add)
        for w in range(3):
            ctx.stt(xc[:, :], xb[:, w:w + TB], cw[:, w:w + 1], xc[:, :], ALU.mult, ALU.add)
        ctx.copy(xcb[:, :], xc[:, :], eng="pool")
        for j in range(TB // 512):
            js = slice(j * 512, (j + 1) * 512)
            p = ps[pi % 4]; pi += 1
            ctx.mm(p[:, :], wa[:, :], xcb[:, js])
            ctx.act(ga[:, js], p[:, :], AF.Sigmoid, bias=ba[:, 0:1], scale=1.0)
            p = ps[pi % 4]; pi += 1
            ctx.mm(p[:, :], wx[:, :], xcb[:, js])
            ctx.act(gx[:, js], p[:, :], AF.Sigmoid, bias=bx[:, 0:1], scale=1.0)
        ctx.act(ga[:, :], ga[:, :], AF.Exp, scale=c1[:, 0:1])
        ctx.tt(t1[:, :], ga[:, :], ga[:, :], ALU.mult, eng="pool")
        ctx.ts(t1[:, :], t1[:, :], -1.0, ALU.mult, 1.0, ALU.add, eng="pool")
        ctx.ts(t1[:, :], t1[:, :], 1e-30, ALU.max)
        ctx.act(t1[:, :], t1[:, :], AF.Sqrt)
        if blk == 0:
            ctx.memset(t1[:, 0:1], 1.0, eng="dve")
        ctx.tt(gx[:, :], gx[:, :], xc[:, :], ALU.mult, eng="pool")
        ctx.tt(gx[:, :], gx[:, :], t1[:, :], ALU.mult)
        h = hh[blk % 2]
        init = 0.0 if blk == 0 else hh[(blk - 1) % 2][:, TB - 1:TB]
        ctx.I("dve", "tensor_tensor_scan", out=h[:, :], data0=ga[:, :], data1=gx[:, :], initial=init,
              op0=ALU.mult, op1=ALU.add)
        ctx.act(t1[:, :], yb[:, :], AF.Square)
        ctx.ts(t1[:, :], t1[:, :], 0.044715, ALU.mult, 1.0, ALU.add, eng="pool")
        ctx.tt(t1[:, :], t1[:, :], yb[:, :], ALU.mult, eng="pool")
        ctx.act(t1[:, :], t1[:, :], AF.Sigmoid, scale=1.5957691216057308)
        ctx.tt(t1[:, :], t1[:, :], yb[:, :], ALU.mult, eng="pool")
        ctx.tt(xc[:, :], t1[:, :], h[:, :], ALU.mult)
        ctx.dma(d_out[:, t0:t0 + TB], xc[:, :])


def mixer_attn(ctx, K, d_z, d_p, d_out):
    PB = 2048
    NP = SEQ // PB
    dil = (1, 4, 16)
    gq = pcol(ctx, d_p["q_norm_g"], 64)
    gk = pcol(ctx, d_p["k_norm_g"], 64)
    ctx.ts(gq[0:64, :], gq[0:64, :], 0.125, ALU.mult)
    eps = ctx.sb([128, 1], F32, "aeps")
    ctx.memset(eps[:, :], RMS_EPS)
    ident = ctx.sb([64, 64], F32, "ident")
    ctx.dma(ident[:, :], d_p["ident64"][:, :])
    sel = ctx.sb([65, 64], F32, "sel")
    ctx.memset(sel[:, :], 0.0)
    ctx.memset(sel[64:65, :], 1.0)
    acc = ctx.sb([65, SEQ], F32, "acc")
    qn = ctx.sb([64, SEQ], BF16, "qn")
    kn = ctx.sb([64, SEQ], BF16, "kn")
    vtok = ctx.sb([128, 64, 65], BF16, "vtok")
    ctx.memset(vtok[:, :, 64:65], 1.0)
    stg = [ctx.sb([64, PB], F32, "astg") for _ in range(3)]
    sq = ctx.sb([64, 512], F32, "asq")
    rs = ctx.sb([64, 512], F32, "ars")
    mst = ctx.sb([128, 256], F32, "mst")
    mst2 = ctx.sb([128, 256], F32, "mst2")
    mask = ctx.sb([128, 256], BF16, "amask")
    ps_s = [ctx.ps([128, 1024], F32, "ps_s") for _ in range(2)]
    ps_o = ctx.ps([65, PB], F32, "ps_o")
    E = [ctx.sb([128, 1024], BF16, "aE") for _ in range(2)]
    for g in range(3):
        d = dil[g]
        ctx.dma(mst[:, :], d_p["biasT"][g, :, :])
        ctx.dma(mst2[:, :], d_p["maskT"][g, :, :])
        ctx.act(mst[:, :], mst[:, :], AF.Exp)
        ctx.tt(mask[:, :], mst[:, :], mst2[:, :], ALU.mult)
        for p in range(NP):
            t0 = p * PB
            for j in range(3):
                ctx.dma(stg[j][:, :], d_z[j, g, :, t0:t0 + PB])
            for j, (dst, gcol) in enumerate(((qn, gq), (kn, gk))):
                for c in range(PB // 512):
                    cs = slice(c * 512, (c + 1) * 512)
                    pp = ps_s[c % 2]
                    ctx.act(sq[:, :], stg[j][:, cs], AF.Square)
                    ctx.mm(pp[0:64, 0:512], K["ones_f"][0:64, 0:64], sq[:, :])
                    ctx.act(rs[:, :], pp[0:64, 0:512], AF.Sqrt, bias=eps[0:64, 0:1], scale=1.0 / 64)
                    ctx.I("dve", "reciprocal", out=rs[:, :], in_=rs[:, :])
                    ctx.stt(dst[:, t0 + c * 512:t0 + (c + 1) * 512], stg[j][:, cs], gcol[0:64, 0:1], rs[:, :],
                            ALU.mult, ALU.mult)
            nloc = PB // (128 * d)
            for half in range(2):
                pp = ps_s[half]
                for b8 in range(8):
                    bi = half * 8 + b8
                    n_l, r = bi // d, bi % d
                    base = n_l * 128 * d + r
                    src = stg[2][:, base:base + 127 * d + 1:d]
                    ctx.I("pe", "transpose", out=pp[:, b8 * 64:(b8 + 1) * 64], in_=src, identity=ident[:, :])
                ctx.copy(vtok[:, p * 16 + half * 8:p * 16 + half * 8 + 8, 0:64],
                         pp.v(pp.t[:, 0:512].rearrange("p (b e) -> p b e", e=64)), eng=("act" if half else "dve"))
        for p in range(NP):
            t0 = p * PB
            nloc = PB // (128 * d)
            for q4 in range(4):
                pp = ps_s[q4 % 2]
                e = E[q4 % 2]
                blks = []
                for b4 in range(4):
                    bi = q4 * 4 + b4
                    n_l, r = bi // d, bi % d
                    n_glob = p * nloc + n_l
                    base = t0 + n_l * 128 * d + r
                    tok = slice(base, base + 127 * d + 1, d)
                    has_prev = n_glob > 0
                    if has_prev:
                        pbase = base - 128 * d
                        ptok = slice(pbase, pbase + 127 * d + 1, d)
                        ctx.mm(pp[:, b4 * 256:b4 * 256 + 128], kn[:, ptok], qn[:, tok])
                    ctx.mm(pp[:, b4 * 256 + 128:b4 * 256 + 256], kn[:, tok], qn[:, tok])
                    blks.append((bi, has_prev))
                allprev = all(hp for _, hp in blks)
                if allprev:
                    ctx.act(e[:, :], pp[:, :], AF.Exp)
                    ctx.tt(e.v(e.t[:, :].rearrange("p (b k) -> p b k", k=256)),
                           e.v(e.t[:, :].rearrange("p (b k) -> p b k", k=256)),
                           mask.v(mask.t[:, :].unsqueeze(1).to_broadcast([128, 4, 256])), ALU.mult)
                else:
                    for b4, (bi, hp) in enumerate(blks):
                        lo = b4 * 256 + (0 if hp else 128)
                        hi = b4 * 256 + 256
                        ctx.act(e[:, lo:hi], pp[:, lo:hi], AF.Exp)
                        ctx.tt(e[:, lo:hi], e[:, lo:hi], mask[:, lo - b4 * 256:256], ALU.mult)
                for b4, (bi, hp) in enumerate(blks):
                    vb = p * 16 + bi
                    o = ps_o[:, bi * 128:(bi + 1) * 128]
                    if hp:
                        ctx.mm(o, vtok[:, vb - d, :], e[:, b4 * 256:b4 * 256 + 128], start=True, stop=False)
                    ctx.mm(o, vtok[:, vb, :], e[:, b4 * 256 + 128:b4 * 256 + 256], start=(not hp), stop=True)
            a_v = acc.t[:, t0:t0 + PB].rearrange("p (n i r) -> p n r i", i=128, r=d)
            o_v = ps_o.t[:, :].rearrange("p (n r i) -> p n r i", i=128, r=d)
            if g == 0:
                ctx.copy(acc.v(a_v), ps_o.v(o_v))
            else:
                ctx.tt(acc.v(a_v), acc.v(a_v), ps_o.v(o_v), ALU.add)
    ost = [ctx.sb([64, 512], F32, "aost") for _ in range(2)]
    for c in range(SEQ // 512):
        cs = slice(c * 512, (c + 1) * 512)
        pp = ps_s[c % 2]
        ctx.mm(pp[0:64, 0:512], sel[:, :], acc[:, cs])
        ctx.I("dve", "reciprocal", out=rs[:, :], in_=pp[0:64, 0:512])
        o = ost[c % 2]
        ctx.tt(o[:, :], acc[0:64, cs], rs[:, :], ALU.mult, eng="pool")
        ctx.dma(d_out[:, cs], o[:, :])


def _t5_bucket_np(dist):
    import math
    max_exact = 16
    d = np.maximum(dist, 0)
    large = max_exact + (np.log(np.maximum(d, 1).astype(np.float32) / max_exact)
                         / np.float32(math.log(2048 / max_exact)) * (32 - max_exact)).astype(np.int32)
    large = np.minimum(large, 31)
    return np.where(d < max_exact, d, large)


def attn_tables(rel_bias, s):
    biasT = np.zeros((3, 128, 256), np.float32)
    maskT = np.zeros((3, 128, 256), np.float32)
    i = np.arange(128)[None, :]
    j = np.arange(128)[:, None]
    for g, d in enumerate((1, 4, 16)):
        h = 4 * g + s
        rel_prev = i + 128 - j
        rel_cur = i - j
        for off, rel in ((0, rel_prev), (128, rel_cur)):
            valid = (rel >= 0) & (rel <= 128)
            bk = _t5_bucket_np(rel * d)
            biasT[g, :, off:off + 128] = np.where(valid, rel_bias[bk, h], 0.0)
            maskT[g, :, off:off + 128] = valid.astype(np.float32)
    return biasT, maskT


def decl_in(ctx, name, shape):
    return ctx.dram(name, shape, F32, "ExternalInput")


def build_B(do_rwkv=True, do_lru=True, do_attn=True):
    nc, ctx = new_ctx()
    K = consts(ctx)
    outs = []
    if do_lru:
        d_zx = decl_in(ctx, "zbx", [128, SEQ])
        d_zy = decl_in(ctx, "zby", [128, SEQ])
        d_p = {"conv_w": decl_in(ctx, "conv_w", [4, 128]), "conv_b": decl_in(ctx, "conv_b", [128]),
               "lru_wa": decl_in(ctx, "lru_wa", [2, 64, 64]), "lru_wx": decl_in(ctx, "lru_wx", [2, 64, 64]),
               "lru_ba": decl_in(ctx, "lru_ba", [128]), "lru_bx": decl_in(ctx, "lru_bx", [128]),
               "lru_lambda": decl_in(ctx, "lru_lambda", [128])}
        d_yB = ctx.dram("yB", [128, SEQ], F32, "ExternalOutput")
        with ctx.scope():
            mixer_lru(ctx, K, d_zx, d_zy, d_p, d_yB)
        outs.append(d_yB)
    if do_attn:
        d_z = decl_in(ctx, "zqkv", [3, 3, 64, SEQ])
        d_p = {"q_norm_g": decl_in(ctx, "q_norm_g", [64]), "k_norm_g": decl_in(ctx, "k_norm_g", [64]),
               "biasT": decl_in(ctx, "biasT", [3, 128, 256]), "maskT": decl_in(ctx, "maskT", [3, 128, 256]),
               "ident64": decl_in(ctx, "ident64", [64, 64])}
        d_yC = ctx.dram("yC", [64, SEQ], F32, "ExternalOutput")
        with ctx.scope():
            mixer_attn(ctx, K, d_z, d_p, d_yC)
        outs.append(d_yC)
    if do_rwkv:
        d_yA = build_rwkv_io(ctx, K)
        outs.append(d_yA)
    ctx.finish(outs)
    return nc


def mixer_inputs(zT, s, W, L, do_rwkv=True, do_lru=True, do_attn=True):
    m = {}
    c = np.ascontiguousarray
    if do_lru:
        cs = slice(s * 128, (s + 1) * 128)
        m["zbx"] = c(zT[1824 + s * 128:1824 + (s + 1) * 128])
        m["zby"] = c(zT[2336 + s * 128:2336 + (s + 1) * 128])
        m["conv_w"] = c(W["conv_w"][L][:, cs])
        m["conv_b"] = c(W["conv_b"][L][cs])
        m["lru_wa"] = c(W["lru_wa"][L][2 * s:2 * s + 2])
        m["lru_wx"] = c(W["lru_wx"][L][2 * s:2 * s + 2])
        m["lru_ba"] = c(W["lru_ba"][L][cs])
        m["lru_bx"] = c(W["lru_bx"][L][cs])
        m["lru_lambda"] = c(W["lru_lambda"][L][cs])
    if do_attn:
        zq = np.empty((3, 3, 64, SEQ), np.float32)
        for j, off in enumerate((2848, 3616, 4384)):
            for g in range(3):
                h = 4 * g + s
                zq[j, g] = zT[off + h * 64:off + (h + 1) * 64]
        m["zqkv"] = zq
        m["q_norm_g"] = c(W["q_norm_g"][L])
        m["k_norm_g"] = c(W["k_norm_g"][L])
        bT, mT = attn_tables(W["rel_bias"], s)
        m["biasT"] = bT
        m["maskT"] = mT
        m["ident64"] = np.eye(64, dtype=np.float32)
    if do_rwkv:
        m.update(rwkv_inputs(zT, s, W, L))
    return m


def rwkv_inputs(zT, s, W, L):
    c = np.ascontiguousarray
    cs = slice(s * 128, (s + 1) * 128)
    m = {}
    m["zrkv"] = np.stack([zT[j * 512 + s * 128:j * 512 + (s + 1) * 128] for j in range(3)])
    m["zl"] = c(zT[1536:1824])
    mu = W["rwkv_mu"][L]
    m["mu_rkv"] = np.stack([mu[j * 512 + s * 128:j * 512 + (s + 1) * 128] for j in range(3)])
    m["mu_l"] = c(mu[1536:1824])
    m["w0"] = c(W["rwkv_w0"][L][cs])
    m["a0"] = c(W["rwkv_a0"][L][cs])
    m["wa_up"] = c(np.concatenate([W["rwkv_w_up"][L][:, cs], W["rwkv_a_up"][L][:, cs]], axis=0))
    m["g_up"] = c(W["rwkv_g_up"][L][:, cs])
    m["k_k"] = c(W["rwkv_k_k"][L][cs])
    m["k_a"] = c(W["rwkv_k_a"][L][cs])
    m["r_k"] = c(W["rwkv_r_k"][L].reshape(-1)[cs])
    m["ln_g"] = c(W["rwkv_ln_g"][L][cs])
    m["ln_b"] = c(W["rwkv_ln_b"][L][cs])
    rm = np.ones((128, 1024), np.float32)
    rm[:, ::64] = 0.0
    m["resetmask"] = rm
    si = np.arange(64)[:, None]
    ti = np.arange(64)[None, :]
    su = (si < ti).astype(np.float32)
    iu = (si <= ti).astype(np.float32)
    sl = (ti < si).astype(np.float32)
    gm = np.concatenate([su, iu, su, iu, sl], axis=1)
    m["gmask"] = c(np.concatenate([gm, gm], axis=0))
    bo = np.zeros((128, 128), np.float32)
    bo[:64, :64] = 1.0
    bo[64:, 64:] = 1.0
    m["blockones"] = bo
    m["ident128"] = np.eye(128, dtype=np.float32)
    return m


def build_rwkv_io(ctx, K):
    d = {"zrkv": decl_in(ctx, "zrkv", [3, 128, SEQ]), "zl": decl_in(ctx, "zl", [288, SEQ]),
         "mu_rkv": decl_in(ctx, "mu_rkv", [3, 128]), "mu_l": decl_in(ctx, "mu_l", [288]),
         "w0": decl_in(ctx, "w0", [128]), "a0": decl_in(ctx, "a0", [128]),
         "wa_up": decl_in(ctx, "wa_up", [128, 128]), "g_up": decl_in(ctx, "g_up", [160, 128]),
         "k_k": decl_in(ctx, "k_k", [128]), "k_a": decl_in(ctx, "k_a", [128]), "r_k": decl_in(ctx, "r_k", [128]),
         "ln_g": decl_in(ctx, "ln_g", [128]), "ln_b": decl_in(ctx, "ln_b", [128]),
         "resetmask": decl_in(ctx, "resetmask", [128, 1024]), "gmask": decl_in(ctx, "gmask", [128, 320]),
         "blockones": decl_in(ctx, "blockones", [128, 128]), "ident128": decl_in(ctx, "ident128", [128, 128])}
    d_yA = ctx.dram("yA", [128, SEQ], F32, "ExternalOutput")
    with ctx.scope():
        mixer_rwkv(ctx, K, d, d_yA)
    return d_yA


def mixer_rwkv(ctx, K, d, d_out):
    TB = 1024
    NCH = TB // CH
    NB = int(os.environ.get('RW_NB', SEQ // TB))
    sb, ps = ctx.sb, ctx.ps
    def col(name, n=128, row0=0):
        t = sb([128, 1], F32, "c_" + name)
        v = d[name].t.rearrange("(p o) -> p o", o=1)
        ctx.dma(t[0:n, 0:1], d[name].v(v[row0:row0 + n, :]))
        return t
    mu = sb([128, 3], F32, "mu")
    ctx.dma(mu[:, :], d["mu_rkv"].v(d["mu_rkv"].t.rearrange("j p -> p j")), allow_slow_non_contiguous=True)
    mu_wa = col("mu_l", 128, 0)
    mu_g1 = col("mu_l", 128, 128)
    mu_g2 = col("mu_l", 32, 256)
    w0, a0 = col("w0"), col("a0")
    k_k, k_a, r_k = col("k_k"), col("k_a"), col("r_k")
    ln_g, ln_b = col("ln_g"), col("ln_b")
    omka = sb([128, 1], F32, "omka")
    ctx.ts(omka[:, :], k_a[:, :], -1.0, ALU.mult, 1.0, ALU.add)
    gneps = sb([128, 1], F32, "gneps")
    ctx.memset(gneps[:, :], GN_EPS)
    wst = sb([128, 128], F32, "rw_wst")
    wa_up = sb([128, 128], BF16, "wa_up")
    ctx.dma(wst[:, :], d["wa_up"][:, :])
    ctx.copy(wa_up[:, :], wst[:, :])
    g_up1 = sb([128, 128], BF16, "g_up1")
    g_up2 = sb([32, 128], BF16, "g_up2")
    wst2 = sb([128, 128], F32, "rw_wst2")
    ctx.dma(wst2[:, :], d["g_up"][0:128, :])
    ctx.copy(g_up1[:, :], wst2[:, :])
    wst3 = sb([32, 128], F32, "rw_wst3")
    ctx.dma(wst3[:, :], d["g_up"][128:160, :])
    ctx.copy(g_up2[:, :], wst3[:, :])
    rmask = sb([128, TB], F32, "rmask")
    ctx.dma(rmask[:, :], d["resetmask"][:, :])
    gmask = sb([128, 320], F32, "gmask")
    ctx.dma(gmask[:, :], d["gmask"][:, :])
    bones = sb([128, 128], F32, "bones")
    ctx.dma(bones[:, :], d["blockones"][:, :])
    ident = sb([128, 128], F32, "ident")
    ctx.dma(ident[:, :], d["ident128"][:, :])
    identbb = sb([128, 128], BF16, "identbb")
    ctx.copy(identbb[:, :], ident[:, :])
    khb = sb([128, TB], BF16, "khb")
    bhb = sb([128, TB], BF16, "bhb")
    vbb = sb([128, TB], BF16, "vbb")
    identb = sb([128, 64], BF16, "identb")
    ctx.copy(identb[0:64, :], ident[0:64, 0:64])
    ctx.copy(identb[64:128, :], ident[64:128, 64:128])
    X = [sb([128, 1 + TB], F32, "X%d" % j) for j in range(3)]
    Xwa = sb([128, 1 + TB], F32, "Xwa")
    Xg1 = sb([128, 1 + TB], F32, "Xg1")
    Xg2 = sb([32, 1 + TB], F32, "Xg2")
    halo = sb([128, 6], F32, "halo")
    ctx.memset(halo[:, :], 0.0)
    tmp = [sb([128, TB], F32, "tmp%d" % j) for j in range(3)]
    twa = sb([128, TB], BF16, "twa")
    sg1 = sb([128, TB], BF16, "sg1")
    sg2 = sb([32, TB], BF16, "sg2")
    logw = sb([128, TB], F32, "logw")
    av = sb([128, TB], F32, "av")
    gv = sb([128, TB], F32, "gv")
    cum = sb([128, TB], F32, "cum")
    kkn = sb([128, TB], F32, "kkn")
    kp = sb([128, TB], F32, "kp")
    bvec = sb([128, TB], F32, "bvec")
    bonus = sb([128, TB], F32, "bonus")
    gC = sb([128, NCH], F32, "gC")
    AR = sb([128, NCH, 2, CH], BF16, "AR")
    Kt = sb([128, TB], BF16, "Kt")
    Bt = sb([128, TB], BF16, "Bt")
    VT = sb([128, NCH, 64], BF16, "VT")
    KhT = sb([128, NCH, 64], BF16, "KhT")
    BhT = sb([128, NCH, 64], BF16, "BhT")
    G = [sb([128, NCH, 320], BF16, "G") for _ in range(2)]
    Pb = [[sb([128, 8, 64], BF16, "Pb") for _ in range(2)] for _ in range(2)]
    PTb = [[sb([128, 8, 64], BF16, "PTb") for _ in range(2)] for _ in range(2)]
    Nb = [[sb([128, 8, 64], BF16, "Nb") for _ in range(2)] for _ in range(2)]
    Nfin = [sb([128, NCH, 64], BF16, "Nfin") for _ in range(2)]
    ysb = sb([128, TB], F32, "ysb")
    S32 = [sb([128, 64], F32, "S32") for _ in range(2)]
    Sbf = [sb([128, 64], BF16, "Sbf") for _ in range(2)]
    for h in range(2):
        ctx.memset(S32[h][:, :], 0.0)
        ctx.memset(Sbf[h][:, :], 0.0)
    WTb = [sb([128, 64], BF16, "WTb") for _ in range(2)]
    UTb = [sb([128, 64], BF16, "UTb") for _ in range(2)]
    bank = [ps([128, 512], F32, "rwps") for _ in range(8)]
    bi = [0]

    def nb2():
        bi[0] += 1
        return bank[6 + bi[0] % 2]

    v3 = lambda t: t.v(t.t[:, :].rearrange("p (c k) -> p c k", k=CH))
    for blk in range(NB):
        t0 = blk * TB
        srcs = [(X[0], d["zrkv"][0, :, t0:t0 + TB], mu[:, 0:1], 128), (X[1], d["zrkv"][1, :, t0:t0 + TB], mu[:, 1:2], 128),
                (X[2], d["zrkv"][2, :, t0:t0 + TB], mu[:, 2:3], 128), (Xwa, d["zl"][0:128, t0:t0 + TB], mu_wa[:, 0:1], 128),
                (Xg1, d["zl"][128:256, t0:t0 + TB], mu_g1[:, 0:1], 128), (Xg2, d["zl"][256:288, t0:t0 + TB], mu_g2[0:32, 0:1], 32)]
        for j, (xt, src, mcol, n) in enumerate(srcs):
            ctx.dma(xt[0:n, 1:1 + TB], src)
            ctx.copy(xt[0:n, 0:1], halo[0:n, j:j + 1], eng="pool")
            ctx.copy(halo[0:n, j:j + 1], xt[0:n, TB:TB + 1], eng="pool")
            tm = tmp[j % 2]
            ctx.tt(tm[0:n, :], xt[0:n, 0:TB], xt[0:n, 1:1 + TB], ALU.subtract, eng="pool")
            ctx.stt(xt[0:n, 1:1 + TB], tm[0:n, :], mcol, xt[0:n, 1:1 + TB], ALU.mult, ALU.add)
        fr, fk, fv = X[0][:, 1:1 + TB], X[1][:, 1:1 + TB], X[2][:, 1:1 + TB]
        ctx.act(twa[0:64, :], Xwa[0:64, 1:1 + TB], AF.Tanh)
        ctx.copy(twa[64:128, :], Xwa[64:128, 1:1 + TB], eng="pool")
        ctx.act(sg1[:, :], Xg1[:, 1:1 + TB], AF.Sigmoid)
        ctx.act(sg2[:, :], Xg2[:, 1:1 + TB], AF.Sigmoid)
        for j in range(TB // 512):
            js = slice(j * 512, (j + 1) * 512)
            p = nb2()
            ctx.mm(p[:, :], wa_up[0:64, :], twa[0:64, js])
            ctx.act(logw[:, js], p[:, :], AF.Sigmoid, bias=w0[:, 0:1], scale=1.0)
            p = nb2()
            ctx.mm(p[:, :], wa_up[64:128, :], twa[64:128, js])
            ctx.act(av[:, js], p[:, :], AF.Sigmoid, bias=a0[:, 0:1], scale=1.0)
            p = nb2()
            ctx.mm(p[:, :], g_up1[:, :], sg1[:, js], start=True, stop=False)
            ctx.mm(p[:, :], g_up2[:, :], sg2[:, js], start=False, stop=True)
            ctx.copy(gv[:, js], p[:, :], eng="act")
        ctx.ts(logw[:, :], logw[:, :], -0.6065306597126334, ALU.mult, eng="pool")
        ctx.I("dve", "tensor_tensor_scan", out=cum[:, :], data0=rmask[:, :], data1=logw[:, :], initial=0.0,
              op0=ALU.mult, op1=ALU.add)
        ctx.act(gC[:, :], cum.v(cum.t[:, CH - 1:TB:CH]), AF.Exp)
        ctx.ts(tmp[0][:, :], fk, k_k[:, 0:1], ALU.mult)
        ctx.act(tmp[1][:, :], tmp[0][:, :], AF.Square)
        for j in range(TB // 512):
            js = slice(j * 512, (j + 1) * 512)
            p = nb2()
            ctx.mm(p[:, :], bones[:, :], tmp[1][:, js])
            ctx.act(tmp[2][:, js], p[:, :], AF.Sqrt)
        ctx.ts(tmp[2][:, :], tmp[2][:, :], 1e-12, ALU.max)
        ctx.I("dve", "reciprocal", out=tmp[2][:, :], in_=tmp[2][:, :])
        ctx.tt(kkn[:, :], tmp[0][:, :], tmp[2][:, :], ALU.mult)
        ctx.ts(tmp[0][:, :], av[:, :], k_a[:, 0:1], ALU.mult, omka[:, 0:1], ALU.add)
        ctx.tt(kp[:, :], fk, tmp[0][:, :], ALU.mult)
        ctx.tt(bvec[:, :], kkn[:, :], av[:, :], ALU.mult, eng="pool")
        ctx.tt(tmp[0][:, :], X[0][:, 1:1 + TB], kp[:, :], ALU.mult, eng="pool")
        ctx.ts(tmp[0][:, :], tmp[0][:, :], r_k[:, 0:1], ALU.mult)
        for j in range(TB // 512):
            js = slice(j * 512, (j + 1) * 512)
            p = nb2()
            ctx.mm(p[:, :], bones[:, :], tmp[0][:, js])
            ctx.tt(bonus[:, js], p[:, :], X[2][:, 1 + j * 512:1 + (j + 1) * 512], ALU.mult)
        AR_a = AR.v(AR.t[:, :, 0, :])
        AR_r = AR.v(AR.t[:, :, 1, :])
        ctx.act(tmp[0][:, :], cum[:, :], AF.Exp)
        ctx.tt(AR_r, v3(X[0]) if False else X[0].v(X[0].t[:, 1:1 + TB].rearrange("p (c k) -> p c k", k=CH)),
               v3(tmp[0]), ALU.mult)
        ctx.tt(tmp[1][:, :], cum[:, :], logw[:, :], ALU.subtract, eng="pool")
        ctx.act(tmp[1][:, :], tmp[1][:, :], AF.Exp)
        ctx.stt(AR_a, v3(kkn), -1.0, v3(tmp[1]), ALU.mult, ALU.mult)
        ctx.act(tmp[2][:, :], cum[:, :], AF.Exp, scale=-1.0)
        ctx.tt(Kt[:, :], kp[:, :], tmp[2][:, :], ALU.mult)
        ctx.tt(Bt[:, :], bvec[:, :], tmp[2][:, :], ALU.mult, eng="pool")
        cumC = cum.v(cum.t[:, :].rearrange("p (c k) -> p c k", k=CH)[:, :, CH - 1:CH].to_broadcast([128, NCH, CH]))
        ctx.tt(v3(tmp[0]), cumC, v3(cum), ALU.subtract)
        ctx.act(tmp[0][:, :], tmp[0][:, :], AF.Exp)
        ctx.tt(khb[:, :], kp[:, :], tmp[0][:, :], ALU.mult)
        ctx.tt(bhb[:, :], bvec[:, :], tmp[0][:, :], ALU.mult, eng="pool")
        ctx.copy(vbb[:, :], X[2][:, 1:1 + TB], eng="pool")
        for src, dst in ((vbb, VT), (khb, KhT), (bhb, BhT)):
            off = 0
            for c4 in range(NCH // 4):
                for h in range(2):
                    hp = slice(h * 64, (h + 1) * 64)
                    p = bank[h * 3 + c4 % 3]
                    pb16 = p.t[hp, 0:128].bitcast(BF16)
                    for cc in range(4):
                        c = c4 * 4 + cc
                        ctx.I("pe", "transpose", out=p.v(pb16[:, cc * 64:(cc + 1) * 64]),
                              in_=src[hp, off + c * CH:off + (c + 1) * CH], identity=identbb[hp, hp])
                    ctx.copy(dst[hp, c4 * 4:(c4 + 1) * 4, :], p.v(pb16.rearrange("p (c k) -> p c k", k=64)),
                             eng=("act" if h else "dve"))
        for c in range(NCH):
            cs = slice(c * CH, (c + 1) * CH)
            for h in range(2):
                hp = slice(h * 64, (h + 1) * 64)
                p = bank[h * 3 + c % 3]
                arr = AR.v(AR.t[hp, c, :, :])
                ctx.mm(p[hp, 0:128], Kt[hp, cs], arr)
                ctx.mm(p[hp, 128:256], Bt[hp, cs], arr)
                ctx.mm(p[hp, 256:320], AR.v(AR.t[hp, c, 0, :]), Bt[hp, cs])
                ctx.tt(G[h][hp, c, :], p[hp, 0:320], gmask[hp, :], ALU.mult)
        for grp in range(NCH // 8):
            chs = slice(grp * 8, (grp + 1) * 8)
            for h in range(2):
                hp = slice(h * 64, (h + 1) * 64)
                ctx.tt(Nb[h][0][hp, :, :], G[h][hp, chs, 128:192],
                       identb.v(identb.t[hp, :].unsqueeze(1).to_broadcast([64, 8, 64])), ALU.add, eng="pool")
            for k in range(1, 6):
                for h in range(2):
                    hp = slice(h * 64, (h + 1) * 64)
                    bPT, bP, bN = bank[h * 3], bank[h * 3 + 1], bank[h * 3 + 2]
                    par, prv = k % 2, (k - 1) % 2
                    for c8 in range(8):
                        c = grp * 8 + c8
                        Pm = G[h][hp, c, 128:192] if k == 1 else Pb[h][prv][hp, c8, :]
                        PTm = G[h][hp, c, 256:320] if k == 1 else PTb[h][prv][hp, c8, :]
                        ctx.mm(bPT[hp, c8 * 64:(c8 + 1) * 64], Pm, PTm)
                        if k < 5:
                            ctx.mm(bP[hp, c8 * 64:(c8 + 1) * 64], PTm, Pm)
                    ctx.copy(PTb[h][par][hp, :, :], bPT.v(bPT.t[hp, :].rearrange("p (c k) -> p c k", k=64)), eng="act")
                    if k < 5:
                        ctx.copy(Pb[h][par][hp, :, :], bP.v(bP.t[hp, :].rearrange("p (c k) -> p c k", k=64)), eng="dve")
                    for c8 in range(8):
                        o = bN[hp, c8 * 64:(c8 + 1) * 64]
                        ctx.mm(o, identb[hp, :], Nb[h][prv][hp, c8, :], start=True, stop=False)
                        ctx.mm(o, PTb[h][par][hp, c8, :], Nb[h][prv][hp, c8, :], start=False, stop=True)
                    dstN = Nfin[h][hp, grp * 8:(grp + 1) * 8, :] if k == 5 else Nb[h][par][hp, :, :]
                    ctx.copy(dstN, bN.v(bN.t[hp, :].rearrange("p (c k) -> p c k", k=64)), eng=("dve" if k % 2 else "act"))
        for c in range(NCH):
            c8 = c % 8
            for h in range(2):
                hp = slice(h * 64, (h + 1) * 64)
                o = bank[h * 3][hp, 0:64]
                ctx.mm(o, AR.v(AR.t[hp, c, 0, :]), Sbf[h][hp, :], start=True, stop=False)
                ctx.mm(o, G[h][hp, c, 0:64], VT[hp, c, :], start=False, stop=True)
                ctx.copy(WTb[h][hp, :], o, eng="act")
            for h in range(2):
                hp = slice(h * 64, (h + 1) * 64)
                o = bank[h * 3 + 1][hp, 0:64]
                ctx.mm(o, Nfin[h][hp, c, :], WTb[h][hp, :])
                ctx.copy(UTb[h][hp, :], o, eng="dve")
            for h in range(2):
                hp = slice(h * 64, (h + 1) * 64)
                o = bank[6 + h][hp, c8 * 64:(c8 + 1) * 64]
                ctx.mm(o, Sbf[h][hp, :], AR.v(AR.t[hp, c, 1, :]), start=True, stop=False)
                ctx.mm(o, UTb[h][hp, :], G[h][hp, c, 192:256], start=False, stop=False)
                ctx.mm(o, VT[hp, c, :], G[h][hp, c, 64:128], start=False, stop=True)
            for h in range(2):
                hp = slice(h * 64, (h + 1) * 64)
                o = bank[h * 3 + 2][hp, 0:64]
                ctx.mm(o, BhT[hp, c, :], UTb[h][hp, :], start=True, stop=False)
                ctx.mm(o, KhT[hp, c, :], VT[hp, c, :], start=False, stop=True)
                ctx.stt(S32[h][hp, :], S32[h][hp, :], gC[hp, c:c + 1], o, ALU.mult, ALU.add)
                ctx.copy(Sbf[h][hp, :], S32[h][hp, :], eng="act")
            if c8 == 7:
                for h in range(2):
                    hp = slice(h * 64, (h + 1) * 64)
                    ctx.copy(ysb[hp, (c // 8) * 512:(c // 8 + 1) * 512], bank[6 + h][hp, :], eng=("act" if h else "dve"))
        for j in range(TB // 512):
            js = slice(j * 512, (j + 1) * 512)
            p = nb2()
            ctx.mm(p[:, :], bones[:, :], ysb[:, js])
            ctx.stt(tmp[0][:, js], p[:, :], -1.0 / 64, ysb[:, js], ALU.mult, ALU.add)
            ctx.act(tmp[1][:, js], tmp[0][:, js], AF.Square)
            p = nb2()
            ctx.mm(p[:, :], bones[:, :], tmp[1][:, js])
            ctx.act(tmp[2][:, js], p[:, :], AF.Sqrt, bias=gneps[:, 0:1], scale=1.0 / 64)
            ctx.I("dve", "reciprocal", out=tmp[2][:, js], in_=tmp[2][:, js])
            ctx.tt(tmp[0][:, js], tmp[0][:, js], tmp[2][:, js], ALU.mult)
            ctx.ts(tmp[0][:, js], tmp[0][:, js], ln_g[:, 0:1], ALU.mult, ln_b[:, 0:1], ALU.add)
            ctx.tt(tmp[0][:, js], tmp[0][:, js], bonus[:, js], ALU.add, eng="pool")
            ctx.tt(tmp[0][:, js], tmp[0][:, js], gv[:, js], ALU.mult)
            ctx.dma(d_out[:, t0 + j * 512:t0 + (j + 1) * 512], tmp[0][:, js])


_CACHE = {}


def _get(name, fn):
    if name not in _CACHE:
        _CACHE[name] = fn()
    return _CACHE[name]


PARAM_KEYS = ["rel_bias", "norm_mix_g", "w_in", "rwkv_mu", "rwkv_w0", "rwkv_w_up", "rwkv_a0", "rwkv_a_up", "rwkv_g_up",
              "rwkv_k_k", "rwkv_k_a", "rwkv_r_k", "rwkv_ln_g", "rwkv_ln_b", "proj_a", "conv_w", "conv_b", "lru_wa",
              "lru_ba", "lru_wx", "lru_bx", "lru_lambda", "proj_b", "q_norm_g", "k_norm_g", "proj_c", "w_out",
              "norm_mlp_g", "mlp_up", "mlp_down"]


def kernel(**inputs):
    W = {k: np.asarray(inputs[k], dtype=np.float32) for k in PARAM_KEYS}
    x = np.asarray(inputs["x"], dtype=np.float32)
    xf = x.reshape(-1, D)
    c = np.ascontiguousarray
    xT = [c(xf[i * NT:(i + 1) * NT].T) for i in range(NCORES)]
    ncA = _get("A", build_A)
    ncB = _get("B", build_B)
    ncC = _get("C", build_C)
    for L in range(2):
        resA = run(ncA, [{"xT": xT[i], "w_in": c(W["w_in"][L]), "g": c(W["norm_mix_g"][L])} for i in range(NCORES)])
        zT = [np.concatenate([resA[b * 4 + q]["zT"] for q in range(4)], axis=1) for b in range(2)]
        resB = run(ncB, [mixer_inputs(zT[i // 4], i % 4, W, L) for i in range(NCORES)])
        in_C = []
        for i in range(NCORES):
            b, q = i // 4, i % 4
            ts = slice(q * NT, (q + 1) * NT)
            m = {"xT": xT[i], "gT": resA[i]["gT"],
                 "yA": c(np.concatenate([resB[b * 4 + s]["yA"][:, ts] for s in range(4)], axis=0)),
                 "yB": c(np.concatenate([resB[b * 4 + s]["yB"][:, ts] for s in range(4)], axis=0)),
                 "yC": c(np.concatenate([resB[b * 4 + s]["yC"][:, ts] for s in range(4)], axis=0)),
                 "norm_mlp_g": c(W["norm_mlp_g"][L])}
            for k in ("proj_a", "proj_b", "proj_c", "w_out", "mlp_up", "mlp_down"):
                m[k] = c(W[k][L])
            in_C.append(m)
        resC = run(ncC, in_C)
        xT = [resC[i]["xT_out"] for i in range(NCORES)]
    out = np.concatenate([t.T for t in xT], axis=0).reshape(x.shape).astype(np.float32)
    return out
```
